# Optimizing a Trainium2 kernel written in Bass

```python
import math
import jax, jax.numpy as jnp
from jax import lax
import numpy as np

D_MODEL = 4096
BATCH = 2
SEQ = 4096
DEPTH = 4

CHUNK = 64
Q_BLOCK = 128
HEAD_DIM = 128
N_MIX_HEADS = D_MODEL // HEAD_DIM
MLA_HEADS = N_MIX_HEADS // 2
FOX_HEADS = N_MIX_HEADS // 4
SB_HEADS = N_MIX_HEADS // 4
MLA_Q_LORA = D_MODEL // 4
MLA_KV_LORA = D_MODEL // 8
MLA_NOPE = HEAD_DIM
MLA_ROPE = 64
MLA_V = HEAD_DIM
ROPE_THETA = 10000.0
FOX_WIDTH = FOX_HEADS * HEAD_DIM
SB_WIDTH = SB_HEADS * HEAD_DIM
N_BRANCH = 3
IN_SPLITS = (MLA_Q_LORA, MLA_KV_LORA, MLA_ROPE,
             FOX_WIDTH, FOX_WIDTH, FOX_WIDTH, FOX_HEADS,
             SB_WIDTH, SB_WIDTH, SB_WIDTH,
             N_BRANCH * D_MODEL)
IN_COLS = sum(IN_SPLITS)
D_FF = 2 * D_MODEL
CONV_WIDTH = 3
EPS = 1e-6
FORGET_BIAS_INIT = 3.0

kernel_name = 'hybrid_mla_fox_stickbreak_convffn'


def _rmsnorm(t, g):
    tf = t.astype(jnp.float32)
    y = tf * lax.rsqrt(jnp.mean(tf * tf, axis=-1, keepdims=True) + EPS)
    return (y * g.astype(jnp.float32)).astype(t.dtype)


def _split(t, sizes):
    idx = np.cumsum(np.array(sizes))[:-1].tolist()
    return jnp.split(t, idx, axis=-1)


def _rope_tables(seq, dtype):
    inv = 1.0 / (ROPE_THETA ** (jnp.arange(0, MLA_ROPE, 2, dtype=jnp.float32) / MLA_ROPE))
    ang = jnp.arange(seq, dtype=jnp.float32)[:, None] * inv[None, :]
    return jnp.cos(ang).astype(dtype), jnp.sin(ang).astype(dtype)


def _apply_rope(t, cos, sin):
    t1, t2 = jnp.split(t, 2, axis=-1)
    return jnp.concatenate([t1 * cos - t2 * sin, t1 * sin + t2 * cos], axis=-1)


def _to_blocks(t):
    b, s = t.shape[:2]
    t = t.reshape((b, s // Q_BLOCK, Q_BLOCK) + t.shape[2:])
    return jnp.moveaxis(t, 1, 0)


def _from_blocks(t):
    nb, b, qb = t.shape[:3]
    return jnp.moveaxis(t, 0, 1).reshape(b, nb * qb, -1)


def _mla_attention(q_nope, q_rope, k_nope, k_rope, v):
    seq = k_nope.shape[1]
    kpos = jnp.arange(seq)
    scale = 1.0 / math.sqrt(MLA_NOPE + MLA_ROPE)

    def block(args):
        qn, qr, i = args
        s = (jnp.einsum('bqhd,bkhd->bhqk', qn, k_nope)
             + jnp.einsum('bqhr,bkr->bhqk', qr, k_rope)).astype(jnp.float32) * scale
        qpos = i * Q_BLOCK + jnp.arange(Q_BLOCK)
        mask = (kpos[None, :] // CHUNK) <= (qpos[:, None] // CHUNK)
        p = jax.nn.softmax(jnp.where(mask, s, -jnp.inf), axis=-1)
        return jnp.einsum('bhqk,bkhd->bqhd', p.astype(v.dtype), v)

    nb = seq // Q_BLOCK
    out = lax.map(block, (_to_blocks(q_nope), _to_blocks(q_rope), jnp.arange(nb)))
    return _from_blocks(out)


def _fox_attention(q, k, v, log_f):
    seq = k.shape[1]
    kpos = jnp.arange(seq)
    scale = 1.0 / math.sqrt(HEAD_DIM)
    c = jnp.cumsum(log_f.astype(jnp.float32), axis=1)
    c_k = jnp.transpose(c, (0, 2, 1))[:, :, None, :]

    def block(args):
        qb, cq, i = args
        s = jnp.einsum('bqhd,bkhd->bhqk', qb, k).astype(jnp.float32) * scale
        s = s + jnp.transpose(cq, (0, 2, 1))[..., None] - c_k
        qpos = i * Q_BLOCK + jnp.arange(Q_BLOCK)
        mask = kpos[None, :] <= qpos[:, None]
        p = jax.nn.softmax(jnp.where(mask, s, -jnp.inf), axis=-1)
        return jnp.einsum('bhqk,bkhd->bqhd', p.astype(v.dtype), v)

    nb = seq // Q_BLOCK
    out = lax.map(block, (_to_blocks(q), _to_blocks(c), jnp.arange(nb)))
    return _from_blocks(out)


def _stick_breaking_attention(q, k, v):
    seq = k.shape[1]
    kpos = jnp.arange(seq)
    scale = 1.0 / math.sqrt(HEAD_DIM)

    def block(args):
        qb, i = args
        z = jnp.einsum('bqhd,bkhd->bhqk', qb, k).astype(jnp.float32) * scale
        qpos = i * Q_BLOCK + jnp.arange(Q_BLOCK)
        strict = kpos[None, :] < qpos[:, None]
        log_one_minus = jnp.where(strict, jax.nn.log_sigmoid(-z), 0.0)
        csum = jnp.cumsum(log_one_minus, axis=-1)
        rest = csum[..., -1:] - csum
        a = jnp.where(strict, jnp.exp(jax.nn.log_sigmoid(z) + rest), 0.0)
        return jnp.einsum('bhqk,bkhd->bqhd', a.astype(v.dtype), v)

    nb = seq // Q_BLOCK
    out = lax.map(block, (_to_blocks(q), jnp.arange(nb)))
    return _from_blocks(out)


def _causal_dwconv(t, w, b):
    seq = t.shape[1]
    tp = jnp.pad(t, ((0, 0), (CONV_WIDTH - 1, 0), (0, 0)))
    out = b
    for i in range(CONV_WIDTH):
        out = out + w[i] * tp[:, i:i + seq]
    return out


def setup_inputs(seed: int = 0) -> dict:
    key = jax.random.key(seed)
    ks = jax.random.split(key, 24)
    L = DEPTH

    def nrm(k, shape, scale):
        return jax.random.normal(k, shape, jnp.float32) * scale

    return {
        'x': nrm(ks[0], (BATCH, SEQ, D_MODEL), 1.0),
        'attn_norm': 1.0 + nrm(ks[1], (L, D_MODEL), 0.01),
        'w_in': nrm(ks[2], (L, D_MODEL, IN_COLS), D_MODEL ** -0.5),
        'b_forget': FORGET_BIAS_INIT + nrm(ks[3], (L, FOX_HEADS), 0.1),
        'b_gate': nrm(ks[4], (L, N_BRANCH * D_MODEL), 0.01),
        'q_norm': 1.0 + nrm(ks[5], (L, MLA_Q_LORA), 0.01),
        'w_uq': nrm(ks[6], (L, MLA_Q_LORA, MLA_HEADS * (MLA_NOPE + MLA_ROPE)), MLA_Q_LORA ** -0.5),
        'kv_norm': 1.0 + nrm(ks[7], (L, MLA_KV_LORA), 0.01),
        'w_ukv': nrm(ks[8], (L, MLA_KV_LORA, MLA_HEADS * (MLA_NOPE + MLA_V)), MLA_KV_LORA ** -0.5),
        'w_br_mla': nrm(ks[9], (L, MLA_HEADS * MLA_V, D_MODEL), (MLA_HEADS * MLA_V) ** -0.5),
        'w_br_fox': nrm(ks[10], (L, FOX_WIDTH, D_MODEL), FOX_WIDTH ** -0.5),
        'w_br_sb': nrm(ks[11], (L, SB_WIDTH, D_MODEL), SB_WIDTH ** -0.5),
        'w_o': nrm(ks[12], (L, D_MODEL, D_MODEL), D_MODEL ** -0.5),
        'ffn_norm': 1.0 + nrm(ks[13], (L, D_MODEL), 0.01),
        'w_ffn_gate': nrm(ks[14], (L, D_MODEL, D_FF), D_MODEL ** -0.5),
        'conv_w': nrm(ks[15], (L, CONV_WIDTH, D_FF), CONV_WIDTH ** -0.5),
        'conv_b': nrm(ks[16], (L, D_FF), 0.01),
        'w_ffn_up': nrm(ks[17], (L, D_MODEL, D_FF), D_MODEL ** -0.5),
        'w_ffn_down': nrm(ks[18], (L, D_FF, D_MODEL), D_FF ** -0.5),
        'final_norm': 1.0 + nrm(ks[19], (D_MODEL,), 0.01),
    }


def reference(x, attn_norm, w_in, b_forget, b_gate, q_norm, w_uq, kv_norm, w_ukv,
              w_br_mla, w_br_fox, w_br_sb, w_o, ffn_norm, w_ffn_gate, conv_w, conv_b,
              w_ffn_up, w_ffn_down, final_norm):
    b, s, _ = x.shape
    cos, sin = _rope_tables(s, x.dtype)
    for l in range(DEPTH):
        h = _rmsnorm(x, attn_norm[l])
        proj = jnp.einsum('bsd,dc->bsc', h, w_in[l])
        (q_lat, kv_lat, k_rope, fq, fk, fv, f_pre,
         sq, sk, sv, g_pre) = _split(proj, IN_SPLITS)

        q = jnp.einsum('bsr,rc->bsc', _rmsnorm(q_lat, q_norm[l]), w_uq[l])
        q = q.reshape(b, s, MLA_HEADS, MLA_NOPE + MLA_ROPE)
        q_nope, q_rope = q[..., :MLA_NOPE], q[..., MLA_NOPE:]
        q_rope = _apply_rope(q_rope, cos[None, :, None, :], sin[None, :, None, :])
        kv = jnp.einsum('bsr,rc->bsc', _rmsnorm(kv_lat, kv_norm[l]), w_ukv[l])
        kv = kv.reshape(b, s, MLA_HEADS, MLA_NOPE + MLA_V)
        k_nope, v_mla = kv[..., :MLA_NOPE], kv[..., MLA_NOPE:]
        k_rope = _apply_rope(k_rope, cos[None], sin[None])
        o_mla = _mla_attention(q_nope, q_rope, k_nope, k_rope, v_mla)

        hs = (b, s, FOX_HEADS, HEAD_DIM)
        log_f = jax.nn.log_sigmoid((f_pre + b_forget[l]).astype(jnp.float32))
        o_fox = _fox_attention(fq.reshape(hs), fk.reshape(hs), fv.reshape(hs), log_f)

        hs = (b, s, SB_HEADS, HEAD_DIM)
        o_sb = _stick_breaking_attention(sq.reshape(hs), sk.reshape(hs), sv.reshape(hs))

        g_a, g_b, g_c = jnp.split(jax.nn.sigmoid(g_pre + b_gate[l]), N_BRANCH, axis=-1)
        merged = (g_a * jnp.einsum('bsc,cd->bsd', o_mla, w_br_mla[l])
                  + g_b * jnp.einsum('bsc,cd->bsd', o_fox, w_br_fox[l])
                  + g_c * jnp.einsum('bsc,cd->bsd', o_sb, w_br_sb[l]))
        x = x + jnp.einsum('bsd,de->bse', merged, w_o[l])

        h = _rmsnorm(x, ffn_norm[l])
        gate = _causal_dwconv(jnp.einsum('bsd,df->bsf', h, w_ffn_gate[l]), conv_w[l], conv_b[l])
        up = jnp.einsum('bsd,df->bsf', h, w_ffn_up[l])
        x = x + jnp.einsum('bsf,fd->bsd', jax.nn.silu(gate) * up, w_ffn_down[l])
    return _rmsnorm(x, final_norm)
```

```python
import math
from contextlib import ExitStack
import numpy as np
import concourse.bass as bass
import concourse.mybir as mybir
from concourse.bass_utils import run_bass_kernel_spmd

F32 = mybir.dt.float32
BF16 = mybir.dt.bfloat16
AF = mybir.ActivationFunctionType
ALU = mybir.AluOpType

D = 4096
SEQ = 4096
DEPTH = 4
IN_COLS = 20040
DFF = 8192
EPS = 1e-6
SC128 = 1.0 / math.sqrt(128.0)
SC192 = 1.0 / math.sqrt(192.0)
NSEM = 90


class Sem:
    def __init__(self, h):
        self.h = h
        self.n = 0


class Prog:
    ENGS = ("sync", "scalar", "vector", "gpsimd", "tensor")

    def __init__(self, nc, stack, sempool):
        self.nc = nc
        self.stack = stack
        self.q = {e: [] for e in self.ENGS}
        self.sempool = sempool
        self.used = 0
        self._nid = 0
        self.final_waits = []

    def uid(self, base):
        Prog._gid = getattr(Prog, "_gid", 0) + 1
        return f"{base}{Prog._gid}"

    def sem(self):
        s = self.sempool[(_stage_state["off"] + self.used) % len(self.sempool)]
        self.used += 1
        assert self.used <= len(self.sempool)
        return s

    def sbuf(self, shape, dt):
        return self.stack.enter_context(self.nc.sbuf_tensor(self.uid("sb"), shape, dt))

    def psum(self, shape, dt):
        return self.stack.enter_context(self.nc.psum_tensor(self.uid("ps"), shape, dt))

    def on(self, eng, fn):
        self.q[eng].append(fn)

    def emit(self):
        fw = list(self.final_waits)

        def _fin(e):
            for s, t in fw:
                e.wait_ge(s.h, t)
        self.q["sync"].append(_fin)
        with self.nc.Block() as block:
            @block.sync
            def _(e):
                for f in self.q["sync"]:
                    f(e)

            @block.scalar
            def _(e):
                for f in self.q["scalar"]:
                    f(e)

            @block.vector
            def _(e):
                for f in self.q["vector"]:
                    f(e)

            @block.gpsimd
            def _(e):
                for f in self.q["gpsimd"]:
                    f(e)

            @block.tensor
            def _(e):
                for f in self.q["tensor"]:
                    f(e)


_stage_state = {"used": 0, "off": 0}


def run_stage(nc, sempool, fn):
    nc.all_engine_barrier()
    with ExitStack() as st:
        P = Prog(nc, st, sempool)
        fn(P)
        P.emit()
        _stage_state["off"] = (_stage_state["off"] + P.used) % len(sempool)
    nc.all_engine_barrier()


class Chain:
    LIMIT = 3000

    def __init__(self, P):
        self.P = P
        self.s = P.sem()
        self.base = self.s.n
        self.prev = None

    def _slot(self, inc):
        if self.s.n - self.base >= self.LIMIT:
            self.prev = (self.s, self.s.n)
            self.s = self.P.sem()
            self.base = self.s.n
        s = self.s
        first = (s.n == self.base)
        n0 = s.n
        s.n += inc
        return s, (None if first else n0), (self.prev if first else None)

    def op(self, en, fn):
        s, n0, prev = self._slot(1)

        def _f(e):
            if n0 is not None:
                e.wait_ge(s.h, n0)
            elif prev is not None:
                e.wait_ge(prev[0].h, prev[1])
            fn(e).then_inc(s.h, 1)
        self.P.on(en, _f)

    def dmas(self, en, pairs):
        k = len(pairs)
        s, n0, prev = self._slot(16 * k)

        def _f(e):
            if n0 is not None:
                e.wait_ge(s.h, n0)
            elif prev is not None:
                e.wait_ge(prev[0].h, prev[1])
            for o, i in pairs:
                e.dma_start(out=o, in_=i).then_inc(s.h, 16)
        self.P.on(en, _f)

    def finish(self):
        self.P.final_waits.append((self.s, self.s.n))


class OutRing:
    def __init__(self, P, dt, n=6, w=512):
        self.P = P
        self.n = n
        self.tiles = [P.sbuf([128, w], dt) for _ in range(n)]
        self.od = P.sem()
        self.od0 = self.od.n
        self.u = 0

    def take(self):
        u = self.u
        self.u += 1
        t = self.tiles[u % self.n]
        need = u - self.n + 1 + 2
        od = self.od
        n = self.n
        od0 = self.od0

        def guard(eng):
            if u >= n:
                eng.wait_ge(od.h, od0 + 16 * min(need, u))
        return t, guard

    def store(self, dst_ap, src_ap, ready_sem, ready_target):
        od = self.od
        od.n += 16

        def _st(e):
            e.wait_ge(ready_sem.h, ready_target)
            e.dma_start(out=dst_ap, in_=src_ap).then_inc(od.h, 16)
        self.P.on("sync", _st)
        self.P.final_waits = [(s, t) for (s, t) in self.P.final_waits if s is not od] + [(od, od.n)]


class LoadRing:
    def __init__(self, P, dt, n=4, w=512):
        self.P = P
        self.n = n
        self.tiles = [P.sbuf([128, w], dt) for _ in range(n)]
        self.ld = [P.sem() for _ in range(n)]
        self.u = 0
        self.free = []

    def load(self, src_ap, rows, cols, consumer_done):
        u = self.u
        self.u += 1
        slot = u % self.n
        t = self.tiles[slot]
        ld = self.ld[slot]
        ld.n += 16
        target = ld.n
        prev = self.free[u - self.n] if u >= self.n else None
        self.free.append(consumer_done)

        def _ld(e):
            if prev is not None:
                e.wait_ge(prev[0].h, prev[1])
            e.dma_start(out=t[0:rows, 0:cols], in_=src_ap).then_inc(ld.h, 16)
        self.P.on("sync", _ld)

        def wait(eng):
            eng.wait_ge(ld.h, target)
        return t[0:rows, 0:cols], wait


def gemm_fm(P, blocks, T, epilogue, wbufs, psum_tiles, prefetch=None, LA=2,
            evac_engs=("scalar", "vector"), kc_per_dma=8):
    nwb = len(wbufs)
    nps = len(psum_tiles)
    w_ld = [P.sem() for _ in range(nwb)]
    ps_full = P.sem()
    pf0 = ps_full.n
    ps_free = {e: P.sem() for e in evac_engs}
    TT = 512
    jobs = []
    cnt = {e: ps_free[e].n for e in evac_engs}
    for b, blk in enumerate(blocks):
        nsub = (blk["nsz"] + 127) // 128
        for m in range(nsub):
            c0 = m * 128
            csz = min(128, blk["nsz"] - c0)
            for t0 in range(0, T, TT):
                j = len(jobs)
                en = evac_engs[j % len(evac_engs)]
                cnt[en] += 1
                jobs.append(dict(j=j, b=b, blk=blk, tag=blk["tag"], r0=blk["r0"] + c0, c0=c0, csz=csz,
                                 t0=t0, tsz=min(TT, T - t0), eng=en, dsem=ps_free[en], dtarget=cnt[en],
                                 ps=psum_tiles[j % nps], first=(m == 0 and t0 == 0)))
    blk_last = {}
    for jb in jobs:
        blk_last[jb["b"]] = jb["j"]
    ld_target = {}
    for b, blk in enumerate(blocks):
        KC = blk["KC"]
        ndma = (KC + kc_per_dma - 1) // kc_per_dma
        sl = w_ld[b % nwb]
        sl.n += 16 * ndma
        ld_target[b] = sl.n
        free_target = (pf0 + blk_last[b - nwb] + 1) if b >= nwb else 0
        wv = blk["w"].rearrange("(kc p) n -> p kc n", p=128)
        prevg = getattr(P, "last_gemm", None) if b < nwb else None

        def _load(eng, wb=wbufs[b % nwb], wv=wv, nsz=blk["nsz"], KC=KC, ndma=ndma, sl=sl, free_target=free_target,
                  prevg=prevg):
            if prevg is not None:
                eng.wait_ge(prevg[0].h, prevg[1])
            if free_target > 0:
                eng.wait_ge(ps_full.h, free_target)
            for d in range(ndma):
                k0 = d * kc_per_dma
                k1 = min(KC, k0 + kc_per_dma)
                eng.dma_start(out=wb[:, k0:k1, 0:nsz], in_=wv[:, k0:k1, 0:nsz]).then_inc(sl.h, 16)
        P.on("gpsimd", _load)
    if prefetch is not None:
        for jb in jobs[:LA]:
            prefetch(jb)
    for jb in jobs:
        j = jb["j"]
        if prefetch is not None and j + LA < len(jobs):
            prefetch(jobs[j + LA])
        b = jb["b"]

        def _mm(eng, jb=jb, j=j, wb=wbufs[b % nwb], sl=w_ld[b % nwb], lt=ld_target[b]):
            if jb["first"]:
                eng.wait_ge(sl.h, lt)
            if j >= nps:
                pj = jobs[j - nps]
                eng.wait_ge(pj["dsem"].h, pj["dtarget"])
            KC = jb["blk"]["KC"]
            xt = jb["blk"]["xt"]
            ps = jb["ps"]
            for kc in range(KC):
                ins = eng.matmul(ps[0:jb["csz"], 0:jb["tsz"]], wb[:, kc, jb["c0"]:jb["c0"] + jb["csz"]],
                                 xt[:, kc, jb["t0"]:jb["t0"] + jb["tsz"]], start=(kc == 0), stop=(kc == KC - 1))
            ins.then_inc(ps_full.h, 1)
        P.on("tensor", _mm)
        body = epilogue(jb)

        def _ev(eng, jb=jb, j=j, body=body):
            eng.wait_ge(ps_full.h, pf0 + j + 1)
            ins = body(eng)
            ins.then_inc(jb["dsem"].h, 1)
        P.on(jb["eng"], _ev)
    ps_full.n = pf0 + len(jobs)
    P.last_gemm = (ps_full, ps_full.n)
    for e in evac_engs:
        ps_free[e].n = cnt[e]
        P.final_waits.append((ps_free[e], cnt[e]))
    return jobs


def col_blocks(w_ap, c0, c1, xt, KC, tag, NB=512, r0=0):
    out = []
    c = c0
    while c < c1:
        n = min(NB, c1 - c)
        out.append(dict(w=w_ap[:, c:c + n], nsz=n, xt=xt, KC=KC, tag=tag, r0=r0 + (c - c0)))
        c += n
    return out


def load_xt(P, xt, src, KC, T, t0, sem, kbase=0):
    sv = src.rearrange("(kc p) t -> p kc t", p=128)
    step = 8
    for k0 in range(0, KC, step):
        k1 = min(KC, k0 + step)
        P.on("sync", lambda e, k0=k0, k1=k1: e.dma_start(out=xt[:, kbase + k0:kbase + k1, 0:T],
                                                        in_=sv[:, k0:k1, t0:t0 + T]).then_inc(sem.h, 16))
        sem.n += 16
    return sem.n


def simple_store_epi(P, rings, dst_fn, evac=None):
    def epi(jb):
        ring, dst_ap = dst_fn(jb)
        ot, guard = ring.take()
        oa = ot[0:jb["csz"], 0:jb["tsz"]]
        ring.store(dst_ap, oa, jb["dsem"], jb["dtarget"])

        def body(eng, jb=jb, oa=oa, guard=guard):
            guard(eng)
            ps = jb["ps"][0:jb["csz"], 0:jb["tsz"]]
            if evac is not None:
                return evac(eng, jb, oa, ps)
            if jb["eng"] == "scalar":
                return eng.activation(out=oa, in_=ps, func=AF.Copy)
            return eng.tensor_copy(out=oa, in_=ps)
        return body
    return epi


def stage_copy(P, dst, src, rows, cols):
    s = P.sem()
    step = 512
    for r in range(0, rows, step):
        P.on("sync", lambda e, r=r: e.dma_start(out=dst[r:r + step, 0:cols], in_=src[r:r + step, 0:cols]).then_inc(s.h, 16))
        s.n += 16
    P.final_waits.append((s, s.n))


def stage_rmsnorm(P, src, dst, g_lay, F, T, C, out_dt):
    KC = F // 128
    TT = 256
    x = P.sbuf([128, KC, TT], F32)
    sq = P.sbuf([128, KC, TT], F32)
    h = P.sbuf([128, KC, TT], out_dt)
    gt = P.sbuf([128, KC], F32)
    rs = P.sbuf([128, TT], F32)
    rs2 = P.sbuf([128, TT], F32)
    ps = P.psum([128, TT], F32)
    on = P.sbuf([128, 128], F32)
    ch = Chain(P)
    sv = src.rearrange("(kc p) t -> p kc t", p=128)
    dv = dst.rearrange("(kc p) t -> p kc t", p=128)
    ch.dmas("sync", [(gt[:, :], g_lay), (on[:, :], C["ones32"][:, :])])
    for t0 in range(0, T, TT):
        ch.dmas("sync", [(x[:, :, :], sv[:, :, t0:t0 + TT])])
        ch.op("scalar", lambda e: e.activation(out=sq[:, :, :], in_=x[:, :, :], func=AF.Square))

        def _mm(e):
            for kc in range(KC):
                ins = e.matmul(ps[:, :], on[:, :], sq[:, kc, :], start=(kc == 0), stop=(kc == KC - 1))
            return ins
        ch.op("tensor", _mm)
        ch.op("vector", lambda e: e.tensor_scalar(out=rs[:, :], in0=ps[:, :], scalar1=1.0 / F, scalar2=EPS,
                                                  op0=ALU.mult, op1=ALU.add))
        ch.op("scalar", lambda e: e.activation(out=rs2[:, :], in_=rs[:, :], func=AF.Sqrt))
        ch.op("vector", lambda e: e.reciprocal(out=rs[:, :], in_=rs2[:, :]))

        def _h(e):
            for kc in range(KC):
                ins = e.scalar_tensor_tensor(out=h[:, kc, :], in0=x[:, kc, :], scalar=gt[:, kc:kc + 1], in1=rs[:, :],
                                             op0=ALU.mult, op1=ALU.mult)
            return ins
        ch.op("vector", _h)
        ch.dmas("sync", [(dv[:, :, t0:t0 + TT], h[:, :, :])])
    ch.finish()


def stage_inproj(P, hT, w_in_l, bg_lay, sc, T):
    TR = 1024
    xt = P.sbuf([128, 32, TR], BF16)
    wbufs = [P.sbuf([128, 32, 512], BF16) for _ in range(2)]
    pst = [P.psum([128, 512], F32) for _ in range(4)]
    o32 = OutRing(P, F32)
    o16 = OutRing(P, BF16)
    bg = P.sbuf([128, 96], F32)
    s_x = P.sem()
    s_c = P.sem()
    P.on("sync", lambda e: e.dma_start(out=bg[:, :], in_=bg_lay).then_inc(s_c.h, 16))
    s_c.n += 16
    P.on("scalar", lambda e, t=s_c.n: e.wait_ge(s_c.h, t))
    segs = [("qlat", 0, 1024), ("kvlat", 1024, 1536), ("krope", 1536, 1600),
            ("fq", 1600, 2624), ("fk", 2624, 3648), ("fv", 3648, 4672), ("fpre", 4672, 4680),
            ("sq", 4680, 5704), ("sk", 5704, 6728), ("sv", 6728, 7752), ("g", 7752, 20040)]
    f32tags = ("qlat", "kvlat", "krope", "fpre")
    prev_last = None
    for tt in range(T // TR):
        t_base = tt * TR
        if prev_last is not None:
            P.on("sync", lambda e, pl=prev_last: e.wait_ge(pl[0].h, pl[1]))
        tgt = load_xt(P, xt, hT, 32, TR, t_base, s_x)
        P.on("tensor", lambda e, tgt=tgt: e.wait_ge(s_x.h, tgt))
        blocks = []
        for tag, c0, c1 in segs:
            blocks += col_blocks(w_in_l, c0, c1, xt, 32, tag)

        def dst_fn(jb, t_base=t_base):
            tag = jb["tag"]
            ring = o32 if tag in f32tags else o16
            return ring, sc[tag][jb["r0"]:jb["r0"] + jb["csz"], t_base + jb["t0"]:t_base + jb["t0"] + jb["tsz"]]

        def evac(eng, jb, oa, ps):
            if jb["tag"] == "g":
                chk = jb["r0"] // 128
                return eng.activation(out=oa, in_=ps, func=AF.Sigmoid, bias=bg[:, chk:chk + 1], scale=1.0)
            return eng.activation(out=oa, in_=ps, func=AF.Copy)
        jobs = gemm_fm(P, blocks, TR, simple_store_epi(P, None, dst_fn, evac), wbufs, pst, evac_engs=("scalar",))
        prev_last = (jobs[-1]["dsem"], jobs[-1]["dtarget"])


def stage_mla_up(P, sc, w_uq_l, w_ukv_l, T):
    xq = P.sbuf([128, 8, T], BF16)
    xk = P.sbuf([128, 4, T], BF16)
    wbufs = [P.sbuf([128, 8, 256], BF16) for _ in range(2)]
    pst = [P.psum([128, 512], F32) for _ in range(4)]
    o32 = OutRing(P, F32)
    o16 = OutRing(P, BF16)
    s_x = P.sem()
    load_xt(P, xq, sc["qn"], 8, T, 0, s_x)
    t2 = load_xt(P, xk, sc["kvn"], 4, T, 0, s_x)
    P.on("tensor", lambda e: e.wait_ge(s_x.h, t2))
    blocks = []
    for h in range(16):
        blocks.append(dict(w=w_uq_l[:, h * 192:(h + 1) * 192], nsz=192, xt=xq, KC=8, tag=("q", h), r0=0))
    for h in range(16):
        blocks.append(dict(w=w_ukv_l[:, h * 256:(h + 1) * 256], nsz=256, xt=xk, KC=4, tag=("kv", h), r0=0))

    def dst_fn(jb):
        kind, h = jb["tag"]
        r0 = jb["r0"]
        cs = slice(jb["t0"], jb["t0"] + jb["tsz"])
        if kind == "q" and r0 == 0:
            return o16, sc["qnope"][h * 128:(h + 1) * 128, cs]
        if kind == "q":
            return o32, sc["qrraw"][h * 64:(h + 1) * 64, cs]
        if r0 == 0:
            return o16, sc["knope"][h * 128:(h + 1) * 128, cs]
        return o16, sc["vmla"][h * 128:(h + 1) * 128, cs]
    gemm_fm(P, blocks, T, simple_store_epi(P, None, dst_fn), wbufs, pst, evac_engs=("scalar",))


def stage_rope(P, src, dst, ngroups, T, C):
    TT = 2048
    A = P.sbuf([128, TT], F32)
    B = P.sbuf([128, TT], F32)
    u = P.sbuf([128, TT], F32)
    v = P.sbuf([128, TT], F32)
    O1 = P.sbuf([128, TT], BF16)
    O2 = P.sbuf([128, TT], BF16)
    cs = P.sbuf([128, T], F32)
    sn = P.sbuf([128, T], F32)
    ch = Chain(P)
    ch.dmas("sync", [(cs[:, :], C["cs4"][:, :]), (sn[:, :], C["sn4"][:, :])])
    sv = src.rearrange("(g two r) t -> g two r t", two=2, r=32)
    dv = dst.rearrange("(g two r) t -> g two r t", two=2, r=32)
    for g0 in range(0, ngroups, 4):
        ng = min(4, ngroups - g0)
        n = ng * 32
        for t0 in range(0, T, TT):
            prs = []
            for gg in range(ng):
                prs.append((A[gg * 32:(gg + 1) * 32, :], sv[g0 + gg, 0, :, t0:t0 + TT]))
                prs.append((B[gg * 32:(gg + 1) * 32, :], sv[g0 + gg, 1, :, t0:t0 + TT]))
            ch.dmas("sync", prs)
            c_ = cs[0:n, t0:t0 + TT]
            s_ = sn[0:n, t0:t0 + TT]
            ch.op("vector", lambda e, c_=c_, n=n: e.tensor_tensor(out=u[0:n, :], in0=A[0:n, :], in1=c_, op=ALU.mult))
            ch.op("vector", lambda e, s_=s_, n=n: e.tensor_tensor(out=v[0:n, :], in0=B[0:n, :], in1=s_, op=ALU.mult))
            ch.op("vector", lambda e, n=n: e.tensor_tensor(out=O1[0:n, :], in0=u[0:n, :], in1=v[0:n, :], op=ALU.subtract))
            ch.op("vector", lambda e, s_=s_, n=n: e.tensor_tensor(out=u[0:n, :], in0=A[0:n, :], in1=s_, op=ALU.mult))
            ch.op("vector", lambda e, c_=c_, n=n: e.tensor_tensor(out=v[0:n, :], in0=B[0:n, :], in1=c_, op=ALU.mult))
            ch.op("vector", lambda e, n=n: e.tensor_tensor(out=O2[0:n, :], in0=u[0:n, :], in1=v[0:n, :], op=ALU.add))
            prs = []
            for gg in range(ng):
                prs.append((dv[g0 + gg, 0, :, t0:t0 + TT], O1[gg * 32:(gg + 1) * 32, :]))
                prs.append((dv[g0 + gg, 1, :, t0:t0 + TT], O2[gg * 32:(gg + 1) * 32, :]))
            ch.dmas("sync", prs)
    ch.finish()


def stage_foxprep(P, sc, bf_l, T, C):
    f = P.sbuf([8, T], F32)
    e1 = P.sbuf([8, T], F32)
    l1 = P.sbuf([8, T], F32)
    cp = P.sbuf([8, T], F32)
    on = P.sbuf([8, T], F32)
    hi = P.sbuf([8, T], BF16)
    hi32 = P.sbuf([8, T], F32)
    lo = P.sbuf([8, T], BF16)
    b = P.sbuf([8, 1], F32)
    nb = P.sbuf([8, 1], F32)
    onec = P.sbuf([128, 1], F32)
    ch = Chain(P)
    ch.dmas("sync", [(f[:, :], sc["fpre"][:, :]), (b[:, :], bf_l), (onec[:, :], C["one_col"][:, :])])
    ch.op("vector", lambda e: e.tensor_scalar(out=nb[:, :], in0=b[:, :], scalar1=-1.0, scalar2=None, op0=ALU.mult))
    ch.op("vector", lambda e: e.memset(on[:, :], 1.0))
    ch.op("scalar", lambda e: e.activation(out=e1[:, :], in_=f[:, :], func=AF.Exp, bias=nb[:, 0:1], scale=-1.0))
    ch.op("scalar", lambda e: e.activation(out=l1[:, :], in_=e1[:, :], func=AF.Ln, bias=onec[0:8, 0:1], scale=1.0))
    ch.op("vector", lambda e: e.tensor_tensor_scan(out=cp[:, :], data0=on[:, :], data1=l1[:, :], initial=0.0,
                                                   op0=ALU.mult, op1=ALU.add))
    ch.op("vector", lambda e: e.tensor_scalar(out=cp[:, :], in0=cp[:, :], scalar1=-math.sqrt(128.0), scalar2=None, op0=ALU.mult))
    ch.op("vector", lambda e: e.tensor_copy(out=hi[:, :], in_=cp[:, :]))
    ch.op("vector", lambda e: e.tensor_copy(out=hi32[:, :], in_=hi[:, :]))
    ch.op("vector", lambda e: e.tensor_tensor(out=lo[:, :], in0=cp[:, :], in1=hi32[:, :], op=ALU.subtract))
    ch.dmas("sync", [(sc["cqhl"][:, 0, :], hi[:, :]), (sc["cqhl"][:, 1, :], lo[:, :])])
    ch.finish()


def stage_attention(P, sc, T, C, heads):
    QT = P.sbuf([128, T], BF16)
    KT = P.sbuf([128, T], BF16)
    VT = P.sbuf([128, T], BF16)
    V = P.sbuf([128, 32, 128], BF16)
    QR = P.sbuf([64, T], BF16)
    KR = P.sbuf([64, T], BF16)
    CQ = P.sbuf([2, T], BF16)
    Pt = P.sbuf([128, 512], BF16)
    Et = P.sbuf([128, 512], F32)
    Lt = P.sbuf([128, 512], F32)
    Xt = P.sbuf([128, 512], F32)
    Acc = P.sbuf([128, 512], F32)
    rD = P.sbuf([128, 512], F32)
    ot = P.sbuf([128, 512], BF16)
    masks = P.sbuf([128, 12, 512], BF16)
    ident = P.sbuf([128, 128], BF16)
    onesb = P.sbuf([128, 128], BF16)
    ones2 = P.sbuf([2, 128], BF16)
    nones2 = P.sbuf([2, 512], BF16)
    tri = P.sbuf([128, 128], F32)
    ones32 = P.sbuf([128, 128], F32)
    S = P.psum([128, 512], F32)
    O = P.psum([128, 512], F32)
    Dn = P.psum([128, 512], F32)
    R = P.psum([128, 512], F32)
    pT = P.psum([128, 4, 128], BF16)
    onec = P.sbuf([128, 1], F32)
    ch = Chain(P)
    ch.dmas("sync", [(onec[:, :], C["one_col"][:, :])])
    ch.dmas("gpsimd", [(masks[:, i, :], C["masks"][i, :, :]) for i in range(12)])
    ch.dmas("gpsimd", [(ident[:, :], C["ident"][:, :]), (onesb[:, :], C["ones32"][:, :]),
                       (ones2[:, :], C["ones32"][0:2, :]), (nones2[:, :], C["nones"][0:2, :])])
    ch.dmas("sync", [(tri[:, :], C["tri"][:, :]), (ones32[:, :], C["ones32"][:, :])])
    mbase = {"mla": 0, "fox": 4, "sb": 8}
    for kind, h in heads:
        if kind == "mla":
            rs = slice(h * 128, (h + 1) * 128)
            prs = [(QT[:, :], sc["qnope"][rs, :]), (KT[:, :], sc["knope"][rs, :]), (VT[:, :], sc["vmla"][rs, :]),
                   (QR[:, :], sc["qr"][h * 64:(h + 1) * 64, :]), (KR[:, :], sc["kr"][:, :])]
            orow = h * 128
            scale = SC192
        elif kind == "fox":
            rs = slice(h * 128, (h + 1) * 128)
            prs = [(QT[:, :], sc["fq"][rs, :]), (KT[:, :], sc["fk"][rs, :]), (VT[:, :], sc["fv"][rs, :]),
                   (CQ[:, :], sc["cqhl"][h, :, :])]
            orow = 2048 + h * 128
            scale = SC128
        else:
            rs = slice(h * 128, (h + 1) * 128)
            prs = [(QT[:, :], sc["sq"][rs, :]), (KT[:, :], sc["sk"][rs, :]), (VT[:, :], sc["sv"][rs, :])]
            orow = 3072 + h * 128
            scale = SC128
        ch.dmas("sync", prs)
        for b4 in range(T // 512):
            def _tr(e, b4=b4):
                for i in range(4):
                    blk = b4 * 4 + i
                    ins = e.transpose(pT[:, i, :], VT[:, blk * 128:(blk + 1) * 128], ident[:, :])
                return ins
            ch.op("tensor", _tr)
            ch.op("vector", lambda e, b4=b4: e.tensor_copy(out=V[:, b4 * 4:(b4 + 1) * 4, :], in_=pT[:, :, :]))
        for qt in range(T // 512):
            qs = slice(qt * 512, (qt + 1) * 512)
            kbs = list(range(4 * qt + 3, -1, -1))
            for si, kb in enumerate(kbs):
                first = (si == 0)
                last = (si == len(kbs) - 1)
                ks = slice(kb * 128, (kb + 1) * 128)
                dj = kb - 4 * qt
                m_ap = masks[:, mbase[kind] + dj, :] if dj >= 0 else None

                def _s(e, ks=ks, qs=qs, kind=kind):
                    ins = e.matmul(S[:, :], KT[:, ks], QT[:, qs], start=True, stop=(kind == "sb"))
                    if kind == "mla":
                        ins = e.matmul(S[:, :], KR[:, ks], QR[:, qs], start=False, stop=True)
                    elif kind == "fox":
                        e.matmul(S[:, :], ones2[:, :], CQ[:, qs], start=False, stop=False)
                        ins = e.matmul(S[:, :], CQ[:, ks], nones2[:, :], start=False, stop=True)
                    return ins
                ch.op("tensor", _s)
                if kind != "sb":
                    ch.op("scalar", lambda e, scale=scale: e.activation(out=Pt[:, :], in_=S[:, :], func=AF.Exp, scale=scale))
                    if m_ap is not None:
                        ch.op("vector", lambda e, m_ap=m_ap: e.tensor_tensor(out=Pt[:, :], in0=Pt[:, :], in1=m_ap, op=ALU.mult))

                    def _pv(e, kb=kb, first=first, last=last):
                        e.matmul(O[:, :], V[:, kb, :], Pt[:, :], start=first, stop=last)
                        return e.matmul(Dn[:, :], onesb[:, :], Pt[:, :], start=first, stop=last)
                    ch.op("tensor", _pv)
                else:
                    ch.op("scalar", lambda e, scale=scale: e.activation(out=Et[:, :], in_=S[:, :], func=AF.Exp, scale=scale))
                    ch.op("scalar", lambda e: e.activation(out=Lt[:, :], in_=Et[:, :], func=AF.Ln,
                                                           bias=onec[:, 0:1], scale=1.0))
                    if m_ap is not None:
                        ch.op("vector", lambda e, m_ap=m_ap: e.tensor_tensor(out=Lt[:, :], in0=Lt[:, :], in1=m_ap, op=ALU.mult))

                    def _r(e, first=first):
                        ins = e.matmul(R[:, :], tri[:, :], Lt[:, :], start=True, stop=first)
                        if not first:
                            ins = e.matmul(R[:, :], ones32[:, :], Acc[:, :], start=False, stop=True)
                        return ins
                    ch.op("tensor", _r)
                    if first:
                        ch.op("vector", lambda e: e.tensor_copy(out=Acc[:, :], in_=Lt[:, :]))
                    else:
                        ch.op("vector", lambda e: e.tensor_tensor(out=Acc[:, :], in0=Acc[:, :], in1=Lt[:, :], op=ALU.add))
                    ch.op("scalar", lambda e: e.activation(out=Xt[:, :], in_=R[:, :], func=AF.Exp, scale=-1.0))
                    ch.op("vector", lambda e: e.tensor_tensor(out=Pt[:, :], in0=Et[:, :], in1=Xt[:, :], op=ALU.mult))
                    if m_ap is not None:
                        ch.op("vector", lambda e, m_ap=m_ap: e.tensor_tensor(out=Pt[:, :], in0=Pt[:, :], in1=m_ap, op=ALU.mult))
                    ch.op("tensor", lambda e, kb=kb, first=first, last=last: e.matmul(O[:, :], V[:, kb, :], Pt[:, :],
                                                                                       start=first, stop=last))
            if kind != "sb":
                ch.op("vector", lambda e: e.reciprocal(out=rD[:, :], in_=Dn[:, :]))
                ch.op("vector", lambda e: e.tensor_tensor(out=ot[:, :], in0=O[:, :], in1=rD[:, :], op=ALU.mult))
            else:
                ch.op("vector", lambda e: e.tensor_copy(out=ot[:, :], in_=O[:, :]))
            ch.dmas("sync", [(sc["oT"][orow:orow + 128, qs], ot[:, :])])
    ch.finish()


def stage_branch(P, sc, wl, T):
    TR = 1024
    xo = P.sbuf([128, 32, TR], BF16)
    wbufs = [P.sbuf([128, 16, 512], BF16) for _ in range(2)]
    pst = [P.psum([128, 512], F32) for _ in range(4)]
    macc = [P.sbuf([128, 512], F32) for _ in range(8)]
    tmp = P.sbuf([128, 512], F32)
    o16 = OutRing(P, BF16)
    gl = LoadRing(P, BF16, n=5)
    s_x = P.sem()
    s_t = P.sem()
    prev_last = None
    gbase = {"a": 0, "b": 4096, "c": 8192}
    for tt in range(T // TR):
        t_base = tt * TR
        if prev_last is not None:
            P.on("sync", lambda e, pl=prev_last: e.wait_ge(pl[0].h, pl[1]))
        tgt = load_xt(P, xo, sc["oT"], 32, TR, t_base, s_x)
        P.on("tensor", lambda e, tgt=tgt: e.wait_ge(s_x.h, tgt))
        blocks = []
        for n0 in range(0, D, 512):
            blocks.append(dict(w=wl["w_br_mla"][:, n0:n0 + 512], nsz=512, xt=xo[:, 0:16, :], KC=16, tag="a", r0=n0))
            blocks.append(dict(w=wl["w_br_fox"][:, n0:n0 + 512], nsz=512, xt=xo[:, 16:24, :], KC=8, tag="b", r0=n0))
            blocks.append(dict(w=wl["w_br_sb"][:, n0:n0 + 512], nsz=512, xt=xo[:, 24:32, :], KC=8, tag="c", r0=n0))

        def prefetch(jb, t_base=t_base):
            r = gbase[jb["tag"]] + jb["r0"]
            src = sc["g"][r:r + jb["csz"], t_base + jb["t0"]:t_base + jb["t0"] + jb["tsz"]]
            jb["g"], jb["gwait"] = gl.load(src, jb["csz"], jb["tsz"], (jb["dsem"], jb["dtarget"]))

        def epi(jb, t_base=t_base):
            mi = (jb["c0"] // 128) * 2 + jb["t0"] // 512
            ma = macc[mi][0:jb["csz"], 0:jb["tsz"]]
            tag = jb["tag"]
            if tag == "c":
                ot, guard = o16.take()
                oa = ot[0:jb["csz"], 0:jb["tsz"]]
                o16.store(sc["merged"][jb["r0"]:jb["r0"] + jb["csz"], t_base + jb["t0"]:t_base + jb["t0"] + jb["tsz"]],
                          oa, jb["dsem"], jb["dtarget"])

            if tag != "a":
                s_t.n += 1
            st_target = s_t.n

            def body(eng, jb=jb):
                jb["gwait"](eng)
                ps = jb["ps"][0:jb["csz"], 0:jb["tsz"]]
                ta = tmp[0:jb["csz"], 0:jb["tsz"]]
                if tag == "a":
                    return eng.tensor_tensor(out=ma, in0=ps, in1=jb["g"], op=ALU.mult)
                eng.tensor_tensor(out=ta, in0=ps, in1=jb["g"], op=ALU.mult).then_inc(s_t.h, 1)
                eng.wait_ge(s_t.h, st_target)
                if tag == "b":
                    return eng.tensor_tensor(out=ma, in0=ma, in1=ta, op=ALU.add)
                guard(eng)
                return eng.tensor_tensor(out=oa, in0=ma, in1=ta, op=ALU.add)
            return body
        jobs = gemm_fm(P, blocks, TR, epi, wbufs, pst, prefetch=prefetch, LA=3, evac_engs=("vector",))
        prev_last = (jobs[-1]["dsem"], jobs[-1]["dtarget"])


def stage_gemm_resid(P, src_bf, w_l, K, xs, T, TR, NB):
    KC = K // 128
    xt = P.sbuf([128, KC, TR], BF16)
    wbufs = [P.sbuf([128, KC, NB], BF16) for _ in range(2)]
    pst = [P.psum([128, 512], F32) for _ in range(4)]
    o32 = OutRing(P, F32)
    xl = LoadRing(P, F32, n=5)
    s_x = P.sem()
    prev_last = None
    for tt in range(T // TR):
        t_base = tt * TR
        if prev_last is not None:
            P.on("sync", lambda e, pl=prev_last: e.wait_ge(pl[0].h, pl[1]))
        tgt = load_xt(P, xt, src_bf, KC, TR, t_base, s_x)
        P.on("tensor", lambda e, tgt=tgt: e.wait_ge(s_x.h, tgt))
        blocks = col_blocks(w_l, 0, D, xt, KC, "o", NB=NB)

        def prefetch(jb, t_base=t_base):
            src = xs[jb["r0"]:jb["r0"] + jb["csz"], t_base + jb["t0"]:t_base + jb["t0"] + jb["tsz"]]
            jb["x"], jb["xwait"] = xl.load(src, jb["csz"], jb["tsz"], (jb["dsem"], jb["dtarget"]))

        def epi(jb, t_base=t_base):
            ot, guard = o32.take()
            oa = ot[0:jb["csz"], 0:jb["tsz"]]
            o32.store(xs[jb["r0"]:jb["r0"] + jb["csz"], t_base + jb["t0"]:t_base + jb["t0"] + jb["tsz"]],
                      oa, jb["dsem"], jb["dtarget"])

            def body(eng, jb=jb, oa=oa, guard=guard):
                jb["xwait"](eng)
                guard(eng)
                return eng.tensor_tensor(out=oa, in0=jb["ps"][0:jb["csz"], 0:jb["tsz"]], in1=jb["x"], op=ALU.add)
            return body
        jobs = gemm_fm(P, blocks, TR, epi, wbufs, pst, prefetch=prefetch, LA=3, evac_engs=("vector",))
        prev_last = (jobs[-1]["dsem"], jobs[-1]["dtarget"])


def stage_ffn_gate(P, sc, w_l, T):
    TR = 1024
    xt = P.sbuf([128, 32, TR], BF16)
    wbufs = [P.sbuf([128, 32, 512], BF16) for _ in range(2)]
    pst = [P.psum([128, 512], F32) for _ in range(4)]
    o32 = OutRing(P, F32)
    s_x = P.sem()
    prev_last = None
    for tt in range(T // TR):
        t_base = tt * TR
        if prev_last is not None:
            P.on("sync", lambda e, pl=prev_last: e.wait_ge(pl[0].h, pl[1]))
            P.on("sync", lambda e, pl=prev_last2: e.wait_ge(pl[0].h, pl[1]))
        tgt = load_xt(P, xt, sc["hT"], 32, TR, t_base, s_x)
        P.on("tensor", lambda e, tgt=tgt: e.wait_ge(s_x.h, tgt))
        blocks = col_blocks(w_l, 0, DFF, xt, 32, "gate")

        def dst_fn(jb, t_base=t_base):
            return o32, sc["gpre"][jb["r0"]:jb["r0"] + jb["csz"], 2 + t_base + jb["t0"]:2 + t_base + jb["t0"] + jb["tsz"]]
        jobs = gemm_fm(P, blocks, TR, simple_store_epi(P, None, dst_fn), wbufs, pst, evac_engs=("scalar",))
        prev_last = (jobs[-1]["dsem"], jobs[-1]["dtarget"])
        prev_last2 = (jobs[-2]["dsem"], jobs[-2]["dtarget"])


def stage_conv(P, sc, cw_lay, cb_lay, T):
    TT = min(2048, T)
    G = P.sbuf([128, TT + 2], F32)
    Cc = P.sbuf([128, TT], F32)
    Sg = P.sbuf([128, TT], BF16)
    cw = P.sbuf([128, 3, 64], F32)
    cb = P.sbuf([128, 64], F32)
    ch = Chain(P)
    ch.dmas("sync", [(cw[:, :, :], cw_lay), (cb[:, :], cb_lay)])
    for rb in range(DFF // 128):
        for t0 in range(0, T, TT):
            ch.dmas("sync", [(G[:, :], sc["gpre"][rb * 128:(rb + 1) * 128, t0:t0 + TT + 2])])
            ch.op("vector", lambda e, rb=rb: e.tensor_scalar(out=Cc[:, :], in0=G[:, 0:TT], scalar1=cw[:, 0, rb:rb + 1],
                                                              scalar2=cb[:, rb:rb + 1], op0=ALU.mult, op1=ALU.add))
            ch.op("vector", lambda e, rb=rb: e.scalar_tensor_tensor(out=Cc[:, :], in0=G[:, 1:TT + 1], scalar=cw[:, 1, rb:rb + 1],
                                                                     in1=Cc[:, :], op0=ALU.mult, op1=ALU.add))
            ch.op("vector", lambda e, rb=rb: e.scalar_tensor_tensor(out=Cc[:, :], in0=G[:, 2:TT + 2], scalar=cw[:, 2, rb:rb + 1],
                                                                     in1=Cc[:, :], op0=ALU.mult, op1=ALU.add))
            ch.op("scalar", lambda e: e.activation(out=Sg[:, :], in_=Cc[:, :], func=AF.Silu))
            ch.dmas("sync", [(sc["sg"][rb * 128:(rb + 1) * 128, t0:t0 + TT], Sg[:, :])])
    ch.finish()


def stage_ffn_up(P, sc, w_l, T):
    TR = 1024
    xt = P.sbuf([128, 32, TR], BF16)
    wbufs = [P.sbuf([128, 32, 512], BF16) for _ in range(2)]
    pst = [P.psum([128, 512], F32) for _ in range(4)]
    o16 = OutRing(P, BF16)
    sl = LoadRing(P, BF16, n=5)
    s_x = P.sem()
    prev_last = None
    for tt in range(T // TR):
        t_base = tt * TR
        if prev_last is not None:
            P.on("sync", lambda e, pl=prev_last: e.wait_ge(pl[0].h, pl[1]))
        tgt = load_xt(P, xt, sc["hT"], 32, TR, t_base, s_x)
        P.on("tensor", lambda e, tgt=tgt: e.wait_ge(s_x.h, tgt))
        blocks = col_blocks(w_l, 0, DFF, xt, 32, "up")

        def prefetch(jb, t_base=t_base):
            src = sc["sg"][jb["r0"]:jb["r0"] + jb["csz"], t_base + jb["t0"]:t_base + jb["t0"] + jb["tsz"]]
            jb["s"], jb["swait"] = sl.load(src, jb["csz"], jb["tsz"], (jb["dsem"], jb["dtarget"]))

        def epi(jb, t_base=t_base):
            ot, guard = o16.take()
            oa = ot[0:jb["csz"], 0:jb["tsz"]]
            o16.store(sc["act"][jb["r0"]:jb["r0"] + jb["csz"], t_base + jb["t0"]:t_base + jb["t0"] + jb["tsz"]],
                      oa, jb["dsem"], jb["dtarget"])

            def body(eng, jb=jb, oa=oa, guard=guard):
                jb["swait"](eng)
                guard(eng)
                return eng.tensor_tensor(out=oa, in0=jb["ps"][0:jb["csz"], 0:jb["tsz"]], in1=jb["s"], op=ALU.mult)
            return body
        jobs = gemm_fm(P, blocks, TR, epi, wbufs, pst, prefetch=prefetch, LA=3, evac_engs=("vector",))
        prev_last = (jobs[-1]["dsem"], jobs[-1]["dtarget"])


def stage_init(P, sc, C):
    z = P.sbuf([128, 2], F32)
    ch = Chain(P)
    ch.op("vector", lambda e: e.memset(z[:, :], 0.0))
    ch.dmas("sync", [(sc["gpre"][rb * 128:(rb + 1) * 128, 0:2], z[:, :]) for rb in range(DFF // 128)])
    ch.finish()


WNAMES = [("w_in", [D, IN_COLS]), ("w_uq", [1024, 3072]), ("w_ukv", [512, 4096]),
          ("w_br_mla", [2048, D]), ("w_br_fox", [1024, D]), ("w_br_sb", [1024, D]), ("w_o", [D, D]),
          ("w_ffn_gate", [D, DFF]), ("w_ffn_up", [D, DFF]), ("w_ffn_down", [DFF, D])]
SMALL = [("attn_norm_l", [128, 32]), ("q_norm_l", [128, 8]), ("kv_norm_l", [128, 4]), ("ffn_norm_l", [128, 32]),
         ("b_gate_l", [128, 96]), ("conv_w_l", [128, 3, 64]), ("conv_b_l", [128, 64]), ("b_forget_l", [8, 1])]


def build_program(nlayers=DEPTH, T=SEQ, stop_after=None, final_norm=True):
    nc = bass.Bass("TRN2", target_bir_lowering=False)
    _stage_state["used"] = 0
    _stage_state["off"] = 0
    xT = nc.dram_tensor("xT", [D, T], F32, kind="ExternalInput").ap()
    yT = nc.dram_tensor("yT", [D, T], F32, kind="ExternalOutput").ap()
    W = {n: [nc.dram_tensor(f"{n}_{l}", s, F32, kind="ExternalInput").ap() for l in range(nlayers)] for n, s in WNAMES}
    S_ = {n: nc.dram_tensor(n, [nlayers] + s, F32, kind="ExternalInput").ap() for n, s in SMALL}
    fin_l = nc.dram_tensor("final_norm_l", [128, 32], F32, kind="ExternalInput").ap()
    C = {}
    for n, s in (("masks", [12, 128, 512]), ("ident", [128, 128]), ("ones32", [128, 128]), ("nones", [128, 512]),
                 ("tri", [128, 128]), ("cs4", [128, T]), ("sn4", [128, T]), ("one_col", [128, 1])):
        C[n] = nc.dram_tensor(n, s, F32, kind="ExternalInput").ap()

    A_attn = [("g", 12288, T, BF16), ("oT", D, T, BF16), ("merged", D, T, BF16),
              ("qnope", 2048, T, BF16), ("knope", 2048, T, BF16), ("vmla", 2048, T, BF16)]
    B_attn = [("qlat", 1024, T, F32), ("kvlat", 512, T, F32), ("krope", 64, T, F32), ("fpre", 8, T, F32),
              ("qrraw", 1024, T, F32),
              ("fq", 1024, T, BF16), ("fk", 1024, T, BF16), ("fv", 1024, T, BF16),
              ("sq", 1024, T, BF16), ("sk", 1024, T, BF16), ("sv", 1024, T, BF16),
              ("qn", 1024, T, BF16), ("kvn", 512, T, BF16), ("qr", 1024, T, BF16),
              ("kr", 64, T, BF16), ("cqhl2", 16, T, BF16)]
    A_ffn = [("gpre", DFF, T + 2, F32)]
    B_ffn = [("sg", DFF, T, BF16), ("act", DFF, T, BF16)]

    def nbytes(spec):
        return sum(((r * c * (4 if d == F32 else 2) + 63) // 64) * 64 for _, r, c, d in spec)
    hT_bytes = D * T * 2
    poolA = nc.dram_tensor("poolA", [(hT_bytes + max(nbytes(A_attn), nbytes(A_ffn))) // 4], F32).ap()
    poolB = nc.dram_tensor("poolB", [max(nbytes(B_attn), nbytes(B_ffn)) // 4], F32).ap()

    def carve(pool, off, r, c, d):
        nb = ((r * c * (4 if d == F32 else 2) + 63) // 64) * 64
        a = pool[off // 4:(off + nb) // 4]
        if d != F32:
            a = a.bitcast(BF16)
        return a[0:r * c].rearrange("(r t) -> r t", t=c), off + nb
    sc = {}
    sc["hT"], base = carve(poolA, 0, D, T, BF16)
    for pool, b0, specs in ((poolA, base, (A_attn, A_ffn)), (poolB, 0, (B_attn, B_ffn))):
        for spec in specs:
            off = b0
            for n, r, c, d in spec:
                sc[n], off = carve(pool, off, r, c, d)
    sc["cqhl"] = sc["cqhl2"].rearrange("(h two) t -> h two t", two=2)
    sc["xs"] = yT
    with ExitStack() as gst:
        sempool = [Sem(gst.enter_context(nc.semaphore(f"sp{i}"))) for i in range(NSEM)]

        cnt = [0]

        def st(fn):
            cnt[0] += 1
            if stop_after is not None and cnt[0] > stop_after:
                return
            run_stage(nc, sempool, fn)
        st(lambda P: stage_copy(P, sc["xs"], xT, D, T))
        heads = [("mla", h) for h in range(16)] + [("fox", h) for h in range(8)] + [("sb", h) for h in range(8)]
        for l in range(nlayers):
            wl = {n: W[n][l] for n, _ in WNAMES}
            st(lambda P: stage_rmsnorm(P, sc["xs"], sc["hT"], S_["attn_norm_l"][l], D, T, C, BF16))
            st(lambda P: stage_inproj(P, sc["hT"], wl["w_in"], S_["b_gate_l"][l], sc, T))
            st(lambda P: stage_rmsnorm(P, sc["qlat"], sc["qn"], S_["q_norm_l"][l], 1024, T, C, BF16))
            st(lambda P: stage_rmsnorm(P, sc["kvlat"], sc["kvn"], S_["kv_norm_l"][l], 512, T, C, BF16))
            st(lambda P: stage_mla_up(P, sc, wl["w_uq"], wl["w_ukv"], T))
            st(lambda P: stage_rope(P, sc["qrraw"], sc["qr"], 16, T, C))
            st(lambda P: stage_rope(P, sc["krope"], sc["kr"], 1, T, C))
            st(lambda P: stage_foxprep(P, sc, S_["b_forget_l"][l], T, C))
            st(lambda P: stage_attention(P, sc, T, C, heads))
            st(lambda P: stage_branch(P, sc, wl, T))
            st(lambda P: stage_gemm_resid(P, sc["merged"], wl["w_o"], D, sc["xs"], T, 1024, 512))
            st(lambda P: stage_rmsnorm(P, sc["xs"], sc["hT"], S_["ffn_norm_l"][l], D, T, C, BF16))
            st(lambda P: stage_init(P, sc, C))
            st(lambda P: stage_ffn_gate(P, sc, wl["w_ffn_gate"], T))
            st(lambda P: stage_conv(P, sc, S_["conv_w_l"][l], S_["conv_b_l"][l], T))
            st(lambda P: stage_ffn_up(P, sc, wl["w_ffn_up"], T))
            st(lambda P: stage_gemm_resid(P, sc["act"], wl["w_ffn_down"], DFF, sc["xs"], T, 512, 256))
        if final_norm:
            st(lambda P: stage_rmsnorm(P, sc["xs"], yT, fin_l, D, T, C, F32))
    return nc


def _lay(v, kc):
    return np.ascontiguousarray(v.reshape(kc, 128).T)


def make_consts(T=SEQ):
    k = np.arange(128)[:, None]
    q = np.arange(512)[None, :]
    masks = np.zeros((12, 128, 512), np.float32)
    for j in range(4):
        masks[0 + j] = ((j * 128 + k) // 64 <= q // 64)
        masks[4 + j] = ((j * 128 + k) <= q)
        masks[8 + j] = ((j * 128 + k) < q)
    jj = np.arange(128)
    tri = (jj[:, None] >= jj[None, :]).astype(np.float32)
    inv = 1.0 / (10000.0 ** (np.arange(0, 64, 2, dtype=np.float32) / 64.0))
    ang = np.arange(T, dtype=np.float32)[None, :] * inv[:, None].astype(np.float32)
    cs = np.cos(ang).astype(np.float32)
    sn = np.sin(ang).astype(np.float32)
    return dict(masks=masks, ident=np.eye(128, dtype=np.float32), ones32=np.ones((128, 128), np.float32),
                nones=-np.ones((128, 512), np.float32), tri=tri,
                cs4=np.ascontiguousarray(np.tile(cs, (4, 1))), sn4=np.ascontiguousarray(np.tile(sn, (4, 1))),
                one_col=np.ones((128, 1), np.float32))


def make_in_map(xb, inp, nlayers=DEPTH, layers=None, xT=None):
    f = lambda a: np.ascontiguousarray(np.asarray(a, dtype=np.float32))
    if layers is None:
        layers = list(range(nlayers))
    m = {"xT": (np.ascontiguousarray(xT) if xT is not None else np.ascontiguousarray(f(xb).T))}
    for n, _ in WNAMES:
        for i, l in enumerate(layers):
            m[f"{n}_{i}"] = f(inp[n][l])
    m["attn_norm_l"] = np.stack([_lay(f(inp["attn_norm"][l]), 32) for l in layers])
    m["q_norm_l"] = np.stack([_lay(f(inp["q_norm"][l]), 8) for l in layers])
    m["kv_norm_l"] = np.stack([_lay(f(inp["kv_norm"][l]), 4) for l in layers])
    m["ffn_norm_l"] = np.stack([_lay(f(inp["ffn_norm"][l]), 32) for l in layers])
    m["b_gate_l"] = np.stack([_lay(f(inp["b_gate"][l]), 96) for l in layers])
    m["conv_w_l"] = np.stack([np.ascontiguousarray(f(inp["conv_w"][l]).reshape(3, 64, 128).transpose(2, 0, 1)) for l in layers])
    m["conv_b_l"] = np.stack([_lay(f(inp["conv_b"][l]), 64) for l in layers])
    m["b_forget_l"] = np.stack([f(inp["b_forget"][l]).reshape(8, 1) for l in layers])
    m["final_norm_l"] = _lay(f(inp["final_norm"]), 32)
    m.update(make_consts())
    return m


def kernel(**inputs):
    x = np.asarray(inputs["x"], dtype=np.float32)
    ncA = build_program(nlayers=2, final_norm=False)
    mid = []
    for b in range(2):
        m = make_in_map(x[b], inputs, layers=[0, 1])
        r = run_bass_kernel_spmd(ncA, [m], core_ids=[0])
        mid.append(np.ascontiguousarray(r.results[0]["yT"]))
        del m, r
    ncB = build_program(nlayers=2, final_norm=True)
    outs = []
    for b in range(2):
        m = make_in_map(None, inputs, layers=[2, 3], xT=mid[b])
        r = run_bass_kernel_spmd(ncB, [m], core_ids=[0])
        outs.append(np.ascontiguousarray(r.results[0]["yT"].T))
        del m, r
    return np.stack(outs).astype(np.float32)
```
